# Optimizing a Trainium2 kernel written in Bass

```python
import math
import jax, jax.numpy as jnp
from jax import lax
import numpy as np

D_MODEL = 1024
BATCH = 4
SEQ = 4096
DEPTH = 2

GRID_W = 64
BLOCK = 128
N_MEM = 256
EPS = 1e-6
ROPE_THETA = 10000.0
NEG_INF = -1e30
F32 = jnp.float32

A_HEADS = 4
A_DK = 64
A_DV = 2 * A_DK
B_HEADS = 8
B_Q_LORA = 256
B_KV_LORA = 128
B_NOPE = 64
B_ROPE = 32
B_DV = 64
C_HEADS = 8
C_KV_HEADS = 2
C_DH = 64
WINDOW = 128
D_HEADS = 8
D_KV_HEADS = 2
D_DH = 64
X_HEADS = 4
X_DH = 128
D_FF = 4 * D_MODEL
REL_BUCKETS = 32
REL_MAX_DIST = 128

N_BRANCH = 4
BRANCH_W = 512

IN_SIZES = (
    A_HEADS * 2 * A_DK,
    A_HEADS * 2 * A_DK,
    A_HEADS * A_DV,
    B_Q_LORA,
    B_KV_LORA,
    B_ROPE,
    C_HEADS * C_DH,
    C_KV_HEADS * C_DH,
    C_KV_HEADS * C_DH,
    D_HEADS * D_DH,
    D_KV_HEADS * D_DH,
    D_KV_HEADS * D_DH,
    N_BRANCH * D_MODEL,
)
IN_WIDTH = 7584

kernel_name = "hybrid_gated_parallel_encoder"


def rms(x, g):
    xf = x.astype(F32)
    y = xf * lax.rsqrt(jnp.mean(xf * xf, axis=-1, keepdims=True) + EPS)
    return (y * g.astype(F32)).astype(x.dtype)


def rope(x, pos):
    d = x.shape[-1]
    inv = ROPE_THETA ** (-jnp.arange(0, d, 2, dtype=F32) / d)
    ang = pos.astype(F32)[:, None] * inv[None, :]
    cos = jnp.concatenate([jnp.cos(ang)] * 2, -1)[:, None, :].astype(x.dtype)
    sin = jnp.concatenate([jnp.sin(ang)] * 2, -1)[:, None, :].astype(x.dtype)
    x1, x2 = x[..., : d // 2], x[..., d // 2:]
    return x * cos + jnp.concatenate([-x2, x1], -1) * sin


def axial_rope(x, row, col):
    h = x.shape[-1] // 2
    return jnp.concatenate([rope(x[..., :h], row), rope(x[..., h:], col)], -1)


def t5_bucket(rel):
    nb = REL_BUCKETS // 2
    max_exact = nb // 2
    n = jnp.abs(rel)
    nf = jnp.maximum(n, 1).astype(F32)
    large = max_exact + (jnp.log(nf / max_exact) / math.log(REL_MAX_DIST / max_exact)
                         * (nb - max_exact)).astype(jnp.int32)
    large = jnp.minimum(large, nb - 1)
    return jnp.where(rel > 0, nb, 0) + jnp.where(n < max_exact, n, large)


def to_blocks(t):
    b, s = t.shape[:2]
    return jnp.moveaxis(t.reshape(b, s // BLOCK, BLOCK, *t.shape[2:]), 1, 0)


def from_blocks(t):
    t = jnp.moveaxis(t, 0, 1)
    return t.reshape(t.shape[0], t.shape[1] * t.shape[2], *t.shape[3:])


def diff_attention(q, k, v, lam, g_qk, g_sub, rel_table, lam_init):
    b, s = q.shape[:2]
    q = rms(q, g_qk[0])
    k = rms(k, g_qk[1])
    lamf = lam.astype(F32)
    lam_full = jnp.exp(jnp.sum(lamf[0] * lamf[1])) - jnp.exp(jnp.sum(lamf[2] * lamf[3])) + lam_init
    scale = A_DK ** -0.5
    kpos = jnp.arange(s)

    def block(args):
        qb, i = args
        qpos = i * BLOCK + jnp.arange(BLOCK)
        bias = rel_table[t5_bucket(kpos[None, :] - qpos[:, None])]
        bias = jnp.transpose(bias, (2, 0, 1)).astype(F32)
        sc = jnp.einsum('bqhmd,bkhmd->bmhqk', qb, k).astype(F32) * scale + bias
        p = jax.nn.softmax(sc, axis=-1)
        w = p[:, 0] - lam_full * p[:, 1]
        return jnp.einsum('bhqk,bkhe->bqhe', w.astype(v.dtype), v)

    o = from_blocks(lax.map(block, (to_blocks(q), jnp.arange(s // BLOCK))))
    o = rms(o, g_sub) * (1.0 - lam_init)
    return o.reshape(b, s, A_HEADS * A_DV)


def mla(c_q, c_kv, k_rope, g_cq, g_ckv, w_uq, w_ukv, g_qk, pos):
    b, s = c_q.shape[:2]
    q = (rms(c_q, g_cq) @ w_uq).reshape(b, s, B_HEADS, B_NOPE + B_ROPE)
    kv = (rms(c_kv, g_ckv) @ w_ukv).reshape(b, s, B_HEADS, B_NOPE + B_DV)
    q_nope = rms(q[..., :B_NOPE], g_qk[0, :B_NOPE])
    q_rope = rope(rms(q[..., B_NOPE:], g_qk[0, B_NOPE:]), pos)
    k_nope = rms(kv[..., :B_NOPE], g_qk[1, :B_NOPE])
    v = kv[..., B_NOPE:]
    k_r = rope(rms(k_rope, g_qk[1, B_NOPE:])[:, :, None, :], pos)[:, :, 0, :]
    scale = (B_NOPE + B_ROPE) ** -0.5

    def block(args):
        qn, qr = args
        sc = (jnp.einsum('bqhd,bkhd->bhqk', qn, k_nope)
              + jnp.einsum('bqhr,bkr->bhqk', qr, k_r)).astype(F32) * scale
        p = jax.nn.softmax(sc, axis=-1)
        return jnp.einsum('bhqk,bkhe->bqhe', p.astype(v.dtype), v)

    o = from_blocks(lax.map(block, (to_blocks(q_nope), to_blocks(q_rope))))
    return o.reshape(b, s, B_HEADS * B_DV)


def window_gqa(q, k, v, g_qk, sink, rel_table):
    b, s = q.shape[:2]
    nb = s // BLOCK
    grp = C_HEADS // C_KV_HEADS
    q = rms(q, g_qk[0])
    k = rms(k, g_qk[1])
    qb = q.reshape(b, nb, BLOCK, C_KV_HEADS, grp, C_DH)

    def band(t):
        tp = jnp.pad(t, ((0, 0), (BLOCK, BLOCK), (0, 0), (0, 0)))
        tp = tp.reshape(b, nb + 2, BLOCK, *t.shape[2:])
        return jnp.concatenate([tp[:, :-2], tp[:, 1:-1], tp[:, 2:]], axis=2)

    kb, vb = band(k), band(v)
    a = jnp.arange(BLOCK)
    c = jnp.arange(3 * BLOCK)
    rel = c[None, :] - BLOCK - a[:, None]
    bias = jnp.transpose(rel_table[t5_bucket(rel)], (2, 0, 1))
    bias = bias.reshape(C_KV_HEADS, grp, BLOCK, 3 * BLOCK).astype(F32)
    kpos = jnp.arange(nb)[:, None] * BLOCK - BLOCK + c[None, :]
    valid = ((jnp.abs(rel) <= WINDOW)[None]
             & (kpos >= 0)[:, None, :] & (kpos < s)[:, None, :])
    sc = jnp.einsum('bnqgrd,bnkgd->bngrqk', qb, kb).astype(F32) * C_DH ** -0.5 + bias
    sc = jnp.where(valid[None, :, None, None], sc, NEG_INF)
    sk = jnp.broadcast_to(sink.reshape(C_KV_HEADS, grp, 1, 1).astype(F32), sc.shape[:-1] + (1,))
    p = jax.nn.softmax(jnp.concatenate([sc, sk], axis=-1), axis=-1)[..., :-1]
    o = jnp.einsum('bngrqk,bnkgd->bnqgrd', p.astype(v.dtype), vb)
    return o.reshape(b, s, C_HEADS * C_DH)


def axial_gqa(q, k, v, g_qk, row, col):
    b, s = q.shape[:2]
    grp = D_HEADS // D_KV_HEADS
    q = axial_rope(rms(q, g_qk[0]), row, col)
    k = axial_rope(rms(k, g_qk[1]), row, col)
    scale = D_DH ** -0.5

    def block(qb):
        qg = qb.reshape(b, BLOCK, D_KV_HEADS, grp, D_DH)
        sc = jnp.einsum('bqgrd,bkgd->bgrqk', qg, k).astype(F32) * scale
        p = jax.nn.softmax(sc, axis=-1)
        o = jnp.einsum('bgrqk,bkgd->bqgrd', p.astype(v.dtype), v)
        return o.reshape(b, BLOCK, D_HEADS * D_DH)

    return from_blocks(lax.map(block, to_blocks(q)))


def memory_xattn(h, mem_n, w_xq, w_xkv, g_qk, w_xo):
    b, s = h.shape[:2]
    m = mem_n.shape[1]
    q = rms((h @ w_xq).reshape(b, s, X_HEADS, X_DH), g_qk[0])
    kv = (mem_n @ w_xkv).reshape(b, m, 2, X_HEADS, X_DH)
    k = rms(kv[:, :, 0], g_qk[1])
    v = kv[:, :, 1]
    sc = jnp.einsum('bqhd,bkhd->bhqk', q, k).astype(F32) * X_DH ** -0.5
    p = jax.nn.softmax(sc, axis=-1)
    o = jnp.einsum('bhqk,bkhd->bqhd', p.astype(v.dtype), v)
    return o.reshape(b, s, X_HEADS * X_DH) @ w_xo


def setup_inputs(seed: int = 0) -> dict:
    key = jax.random.key(seed)
    ks = iter(jax.random.split(key, 40))
    L = DEPTH

    def w(shape, fan_in):
        return jax.random.normal(next(ks), shape, F32) * fan_in ** -0.5

    def gain(shape):
        return 1.0 + 0.02 * jax.random.normal(next(ks), shape, F32)

    def small(shape, sc):
        return sc * jax.random.normal(next(ks), shape, F32)

    return {
        "x": jax.random.normal(next(ks), (BATCH, SEQ, D_MODEL), F32),
        "mem": jax.random.normal(next(ks), (BATCH, N_MEM, D_MODEL), F32),
        "rel_bias": small((REL_BUCKETS, A_HEADS + C_HEADS), 0.1),
        "g_mix": gain((L, D_MODEL)),
        "w_in": w((L, D_MODEL, IN_WIDTH), D_MODEL),
        "lam": small((L, 4, A_DK), 0.1),
        "g_qk_a": gain((L, 2, A_DK)),
        "g_sub_a": gain((L, A_DV)),
        "g_cq": gain((L, B_Q_LORA)),
        "g_ckv": gain((L, B_KV_LORA)),
        "w_uq": w((L, B_Q_LORA, B_HEADS * (B_NOPE + B_ROPE)), B_Q_LORA),
        "w_ukv": w((L, B_KV_LORA, B_HEADS * (B_NOPE + B_DV)), B_KV_LORA),
        "g_qk_b": gain((L, 2, B_NOPE + B_ROPE)),
        "g_qk_c": gain((L, 2, C_DH)),
        "sink_c": small((L, C_HEADS), 0.5),
        "g_qk_d": gain((L, 2, D_DH)),
        "w_branch": w((L, N_BRANCH, BRANCH_W, D_MODEL), BRANCH_W),
        "w_out": w((L, D_MODEL, D_MODEL), D_MODEL),
        "g_x": gain((L, D_MODEL)),
        "g_mem": gain((L, D_MODEL)),
        "w_xq": w((L, D_MODEL, X_HEADS * X_DH), D_MODEL),
        "w_xkv": w((L, D_MODEL, 2 * X_HEADS * X_DH), D_MODEL),
        "g_qk_x": gain((L, 2, X_DH)),
        "w_xo": w((L, X_HEADS * X_DH, D_MODEL), X_HEADS * X_DH),
        "g_mlp": gain((L, D_MODEL)),
        "w_up": w((L, D_MODEL, D_FF), D_MODEL),
        "w_down": w((L, D_FF, D_MODEL), D_FF),
    }


def reference(x, mem, rel_bias, g_mix, w_in, lam, g_qk_a, g_sub_a, g_cq, g_ckv, w_uq, w_ukv,
              g_qk_b, g_qk_c, sink_c, g_qk_d, w_branch, w_out, g_x, g_mem, w_xq, w_xkv,
              g_qk_x, w_xo, g_mlp, w_up, w_down):
    b, s, _ = x.shape
    rows = s // GRID_W
    pos = jnp.arange(s)
    row = jnp.repeat(jnp.arange(rows), GRID_W)
    col = jnp.tile(jnp.arange(GRID_W), rows)
    split_points = np.cumsum(IN_SIZES)[:-1].tolist()
    rel_a = rel_bias[:, :A_HEADS]
    rel_c = rel_bias[:, A_HEADS:]

    for layer in range(DEPTH):
        lam_init = 0.8 - 0.6 * math.exp(-0.3 * layer)
        h = rms(x, g_mix[layer])
        (aq, ak, av, bcq, bckv, bkr, cq, ck, cv, dq, dk, dv, gate) = jnp.split(
            h @ w_in[layer], split_points, axis=-1)
        o_a = diff_attention(aq.reshape(b, s, A_HEADS, 2, A_DK), ak.reshape(b, s, A_HEADS, 2, A_DK),
                             av.reshape(b, s, A_HEADS, A_DV), lam[layer], g_qk_a[layer],
                             g_sub_a[layer], rel_a, lam_init)
        o_b = mla(bcq, bckv, bkr, g_cq[layer], g_ckv[layer], w_uq[layer], w_ukv[layer],
                  g_qk_b[layer], pos)
        o_c = window_gqa(cq.reshape(b, s, C_HEADS, C_DH), ck.reshape(b, s, C_KV_HEADS, C_DH),
                         cv.reshape(b, s, C_KV_HEADS, C_DH), g_qk_c[layer], sink_c[layer], rel_c)
        o_d = axial_gqa(dq.reshape(b, s, D_HEADS, D_DH), dk.reshape(b, s, D_KV_HEADS, D_DH),
                        dv.reshape(b, s, D_KV_HEADS, D_DH), g_qk_d[layer], row, col)
        o = jnp.stack([o_a, o_b, o_c, o_d], axis=2)
        br = jnp.einsum('bsmc,mcd->bsmd', o, w_branch[layer])
        g = jax.nn.sigmoid(gate.reshape(b, s, N_BRANCH, D_MODEL))
        x = x + jnp.einsum('bsmd,bsmd->bsd', g, br) @ w_out[layer]
        x = x + memory_xattn(rms(x, g_x[layer]), rms(mem, g_mem[layer]), w_xq[layer],
                             w_xkv[layer], g_qk_x[layer], w_xo[layer])
        u = rms(x, g_mlp[layer]) @ w_up[layer]
        x = x + jnp.square(jax.nn.relu(u)) @ w_down[layer]
    return x
```

```python
import math
from contextlib import ExitStack

import numpy as np
import concourse.bass as bass
import concourse.mybir as mybir
from concourse.bass_utils import run_bass_kernel_spmd

F32 = mybir.dt.float32
BF16 = mybir.dt.bfloat16
AF = mybir.ActivationFunctionType
ALU = mybir.AluOpType

D = 1024
SEQ = 4096
NQ = 2048
NK = 4096
NMEM = 256
EPS = 1e-6
NEG = -1e30
DEPTH = 2
C_AQ, C_AK, C_AV, C_BCQ, C_BCKV, C_BKR, C_CQ, C_CK, C_CV, C_DQ, C_DK, C_DV, C_G = (
    0, 512, 1024, 1536, 1792, 1920, 1952, 2464, 2592, 2720, 3232, 3360, 3488)

COMPUTE = ("pe", "act", "dve", "pool")
import os
ROPE_ENG = os.environ.get("ROPE_ENG", "dve")


class Op:
    __slots__ = ("eng", "fn", "reads", "writes", "chan", "deps", "sig", "sem", "val", "idx")

    def __init__(self, eng, fn, reads, writes, chan):
        self.eng, self.fn, self.reads, self.writes, self.chan = eng, fn, reads, writes, chan
        self.deps = []
        self.sig = False
        self.sem = None
        self.val = 0


class Sched:
    def __init__(self, nc, same_engine_sync=True):
        self.nc = nc
        self.ops = []
        self.last_writer = {}
        self.readers = {}
        self.same_engine_sync = same_engine_sync
        self.chan_last = {}
        self.eng_last = {}
        self.barrier_op = None
        self.rr = {"sp": 0, "pool": 0, "act": 0}
        self.nchan = {"sp": 24, "pool": 16, "act": 4}

    def add(self, eng, fn, reads=(), writes=(), chan=None):
        if chan is not None:
            chan = "%s%02d" % (eng, self.rr[eng] % self.nchan[eng])
            self.rr[eng] += 1
        op = Op(eng, fn, tuple(reads), tuple(writes), chan)
        op.idx = len(self.ops)
        deps = set()
        for r in op.reads:
            w = self.last_writer.get(r)
            if w is not None:
                deps.add(w)
        for r in op.writes:
            w = self.last_writer.get(r)
            if w is not None:
                deps.add(w)
            for rd in self.readers.get(r, ()):
                deps.add(rd)
        if chan is not None:
            prev = self.chan_last.get(chan)
            if prev is not None:
                deps.add(prev)
            self.chan_last[chan] = op.idx
        if self.barrier_op is not None:
            deps.add(self.barrier_op)
        deps.discard(op.idx)
        keep = []
        for d in sorted(deps):
            p = self.ops[d]
            if p.chan is None and p.eng == eng and chan is None:
                if eng == "pe" or not self.same_engine_sync:
                    continue
            keep.append(d)
            p.sig = True
        op.deps = keep
        for r in op.writes:
            self.last_writer[r] = op.idx
            self.readers[r] = []
        for r in op.reads:
            if r not in op.writes:
                self.readers.setdefault(r, []).append(op.idx)
        self.ops.append(op)
        if chan is None:
            self.eng_last[eng] = op.idx
        return op

    def barrier(self):
        deps = set(self.eng_last.values()) | set(self.chan_last.values())
        if self.barrier_op is not None:
            deps.add(self.barrier_op)
        op = Op("sp", lambda e: e.nop(), (), (), None)
        op.idx = len(self.ops)
        op.deps = sorted(deps)
        for d in op.deps:
            self.ops[d].sig = True
        op.sig = True
        self.ops.append(op)
        self.barrier_op = op.idx
        self.last_writer = {}
        self.readers = {}

    def dma(self, eng, out, in_, reads, writes, chan):
        return self.add(eng, lambda e: e.dma_start(out=out, in_=in_), reads, writes, chan)

    def emit(self):
        nc = self.nc
        with ExitStack() as es:
            names = COMPUTE + ("sp",)
            eng_sem = {e: es.enter_context(nc.semaphore("s_" + e)) for e in names}
            chans = sorted({op.chan for op in self.ops if op.chan is not None}, key=str)
            chan_sem = {c: es.enter_context(nc.semaphore("c%d" % i)) for i, c in enumerate(chans)}
            cnt = {e: 0 for e in names}
            ccnt = {c: 0 for c in chans}
            for op in self.ops:
                if op.chan is not None:
                    ccnt[op.chan] += 16
                    op.sem, op.val = chan_sem[op.chan], ccnt[op.chan]
                    op.sig = True
                elif op.sig:
                    cnt[op.eng] += 1
                    op.sem, op.val = eng_sem[op.eng], cnt[op.eng]
            final = [(chan_sem[c], ccnt[c]) for c in chans]
            block = es.enter_context(nc.Block())
            ops = self.ops

            def run(engname):
                def body(e):
                    waited = {}
                    for op in ops:
                        if op.eng != engname:
                            continue
                        for d in op.deps:
                            p = ops[d]
                            key = p.sem.num
                            if waited.get(key, 0) >= p.val:
                                continue
                            e.wait_ge(p.sem, p.val)
                            waited[key] = p.val
                        ins = op.fn(e)
                        if op.sig:
                            ins.then_inc(op.sem, 16 if op.chan is not None else 1)
                    if engname == "sp":
                        for s_, v in final:
                            if v > 0 and waited.get(s_.num, 0) < v:
                                e.wait_ge(s_, v)
                return body

            block.tensor(run("pe"))
            block.scalar(run("act"))
            block.vector(run("dve"))
            block.gpsimd(run("pool"))
            block.sync(run("sp"))
        self.stats = dict(n_ops=len(self.ops), sems=len(chans) + 5, sig=dict(cnt))


class Ring:
    def __init__(self, nc, es, name, shape, dtype, n):
        self.t = [es.enter_context(nc.sbuf_tensor(name + "_%d" % i, shape, dtype)) for i in range(n)]
        self.names = [name + "_%d" % i for i in range(n)]
        self.i = 0

    def next(self):
        k = self.i % len(self.t)
        self.i += 1
        return self.t[k], self.names[k]


def _t5_bucket_table():
    import jax
    import jax.numpy as jnp
    with jax.default_device(jax.devices("cpu")[0]):
        rel = jnp.arange(-(SEQ - 1), SEQ, dtype=jnp.int32)
        nb = 16
        max_exact = 8
        n = jnp.abs(rel)
        nf = jnp.maximum(n, 1).astype(jnp.float32)
        large = max_exact + (jnp.log(nf / max_exact) / math.log(128 / max_exact)
                             * (nb - max_exact)).astype(jnp.int32)
        large = jnp.minimum(large, nb - 1)
        out = jnp.where(rel > 0, nb, 0) + jnp.where(n < max_exact, n, large)
        return np.asarray(out)


def _positions(c):
    blk = np.arange(128)
    qpos = np.concatenate([(2 * i + c) * 128 + blk for i in range(16)])
    kpos = np.concatenate([(2 * i) * 128 + blk for i in range(16)] + [(2 * i + 1) * 128 + blk for i in range(16)])
    return qpos, kpos


def _rope_tables(pos, d):
    inv = (10000.0 ** (-np.arange(0, d, 2, dtype=np.float32) / d)).astype(np.float32)
    ang = pos.astype(np.float32)[:, None] * inv[None, :]
    cos = np.concatenate([np.cos(ang)] * 2, -1).astype(np.float32)
    sin = np.concatenate([np.sin(ang)] * 2, -1).astype(np.float32)
    sgn = np.concatenate([-np.ones(d // 2), np.ones(d // 2)]).astype(np.float32)
    return cos, sin * sgn[None, :]


def _class_key(j, kc):
    return ("e", kc - 4 * j) if kc < 16 else ("o", kc - 16 - 4 * j)


_HOST_CACHE = {}


def _host_tables():
    if "t" in _HOST_CACHE:
        return _HOST_CACHE["t"]
    bucket = _t5_bucket_table()
    pos = [_positions(c) for c in range(2)]
    rel = {}
    classes = {}
    for j in range(4):
        for kc in range(32):
            classes.setdefault(_class_key(j, kc), []).append((j, kc))
    for c in range(2):
        qpos, kpos = pos[c]
        for key, mem in classes.items():
            mats = [kpos[kc * 128:(kc + 1) * 128][:, None] - qpos[j * 512:(j + 1) * 512][None, :] for j, kc in mem]
            for m_ in mats[1:]:
                assert np.array_equal(m_, mats[0])
            rel[(c, key)] = mats[0]
    a_tile_keys = []
    c_tile_keys = []
    for key in sorted(classes.keys()):
        bks = [bucket[rel[(c, key)] + SEQ - 1] for c in range(2)]
        if any(b.min() != b.max() for b in bks):
            a_tile_keys.append(key)
        if any((np.abs(rel[(c, key)]) <= 128).any() for c in range(2)):
            c_tile_keys.append(key)
    t = dict(bucket=bucket, pos=pos, rel=rel, classes=classes, a_tile_keys=a_tile_keys, c_tile_keys=c_tile_keys)
    _HOST_CACHE["t"] = t
    return t


def _consts():
    m = np.zeros((6, 128, 128), np.float32)
    m[0] = np.eye(128)
    m[1, :64, :64] = 1 / 64
    m[1, 64:, 64:] = 1 / 64
    m[2, :64, :64] = 1 / 64
    m[2, 64:96, 64:96] = 1 / 32
    m[3] = 1 / 128
    m[4] = 1 / 256
    m[5, :32, :32] = 1 / 32
    return m


class Builder:
    def __init__(self, nc, layers, fused, debug=()):
        self.nc = nc
        self.s = Sched(nc)
        self.layers = layers
        self.fused = fused
        self.debug = set(debug)
        self.T = _host_tables()
        self.nbank = 0
        self.uid = 0

    def din(self, name, shape, dt=F32):
        return self.nc.dram_tensor(name, list(shape), dt, kind="ExternalInput").ap()

    def dscr(self, name, shape, dt):
        kind = "ExternalOutput" if name in self.debug else "Internal"
        return self.nc.dram_tensor(name, list(shape), dt, kind=kind).ap()

    def declare(self):
        T = self.T
        self.xo_in = self.din("xo", [NQ, D])
        self.xf_in = self.din("xf", [NK, D])
        self.mem_in = self.din("mem", [NMEM, D])
        self.consts = self.din("consts", [6, 128, 128])
        self.rel_bias = self.din("rel_bias", [32, 12])
        W = {}
        for nm, shp in [("g_mix", [2, D]), ("w_in", [2, D, 7584]), ("lam", [2, 4, 64]), ("g_qk_a", [2, 2, 64]),
                        ("g_sub_a", [2, 128]), ("g_cq", [2, 256]), ("g_ckv", [2, 128]), ("w_uq", [2, 256, 768]),
                        ("w_ukv", [2, 128, 1024]), ("g_qk_b", [2, 2, 96]), ("g_qk_c", [2, 2, 64]),
                        ("sink_c", [2, 8]), ("g_qk_d", [2, 2, 64]), ("w_branch", [2, 4, 512, D]),
                        ("w_out", [2, D, D]), ("g_x", [2, D]), ("g_mem", [2, D]), ("w_xq", [2, D, 512]),
                        ("w_xkv", [2, D, 1024]), ("g_qk_x", [2, 2, 128]), ("w_xo", [2, 512, D]),
                        ("g_mlp", [2, D]), ("w_up", [2, D, 4096]), ("w_down", [2, 4096, D])]:
            W[nm] = self.din(nm, shp)
        self.W = W
        self.rt_dq = self.din("rt_dq", [2, 128, NQ])
        self.rt_dk = self.din("rt_dk", [2, 128, NK])
        self.rt_bq = self.din("rt_bq", [2, 96, NQ])
        self.rt_bk = self.din("rt_bk", [2, 32, NK])
        self.bca = self.din("bca", [128, 4 * 4 * 32])
        self.nta = self.din("nta", [4, len(T["a_tile_keys"]), 128, 512])
        self.ntc = self.din("ntc", [8, len(T["c_tile_keys"]), 128, 512])
        self.out = self.nc.dram_tensor("out", [NQ, D], F32, kind="ExternalOutput").ap()
        self.AQs = self.dscr("AQs", [512, NQ], BF16)
        self.AKs = self.dscr("AKs", [512, NK], BF16)
        self.VAs = self.dscr("VAs", [NK, 512], BF16)
        self.BQs = self.dscr("BQs", [8, 96, NQ], BF16)
        self.BKs = self.dscr("BKs", [8, 96, NK], BF16)
        self.VBs = self.dscr("VBs", [NK, 512], BF16)
        self.CQs = self.dscr("CQs", [512, NQ], BF16)
        self.CKs = self.dscr("CKs", [128, NK], BF16)
        self.DQs = self.dscr("DQs", [512, NQ], BF16)
        self.DKs = self.dscr("DKs", [128, NK], BF16)
        self.VCDs = self.dscr("VCDs", [NK, 256], BF16)
        self.Os = self.dscr("Os", [4, 512, NQ], BF16)
        self.HTOs = self.dscr("HTOs", [D, NQ], BF16)
        self.HTFd = self.dscr("HTFd", [D, NK], BF16) if "HTFd" in self.debug else None
        if self.fused:
            self.XO = self.dscr("XO", [NQ, D], F32)
            self.XF = self.dscr("XF", [NK, D], F32)

    def u(self, p="r"):
        self.uid += 1
        return "%s%d" % (p, self.uid)

    def bank(self):
        k = self.bankpool[self.nbank % len(self.bankpool)]
        self.nbank += 1
        return self.PS[k], "PB%d" % k

    def wload(self, dst, dst_res, src, chan, eng="pool"):
        self.s.dma(eng, dst, src.rearrange("(k p) c -> p k c", p=128), [], [dst_res], chan)

    def setup_globals(self, es):
        nc, s = self.nc, self.s
        self.PS = [es.enter_context(nc.psum_tensor("PSB%d" % i, [128, 512], F32)) for i in range(8)]
        self.bankpool = list(range(8))
        self.CM = es.enter_context(nc.sbuf_tensor("CM", [128, 6, 128], F32))
        self.IDB = es.enter_context(nc.sbuf_tensor("IDB", [128, 128], BF16))
        self.ONESB = es.enter_context(nc.sbuf_tensor("ONESB", [128, 128], BF16))
        s.dma("sp", self.CM[:], self.consts.rearrange("m p c -> p m c"), [], ["CM"], "cCM")
        s.add("dve", lambda e: e.tensor_copy(self.IDB[:], self.CM[:, 0, :]), ["CM"], ["IDB"])
        s.add("dve", lambda e: e.memset(self.ONESB[:], 1.0), [], ["ONESB"])

    def om(self, k, rows):
        return self.CM[0:rows, k, 0:rows]

    def load_layer_params(self, es, l):
        nc, s, W = self.nc, self.s, self.W
        G = es.enter_context(nc.sbuf_tensor("GC%d" % l, [128, 32], F32))
        self.G = G
        cols = {}
        n = [0]

        def col(name, pieces):
            k = n[0]
            n[0] += 1
            cols[name] = k
            for r0, ap in pieces:
                ln = ap.shape[0]
                s.dma("sp", G[r0:r0 + ln, k:k + 1], ap.rearrange("(p o) -> p o", o=1), [], ["GC"], self.u("cg"))
            return k

        ga, gb, gc, gd, gx = W["g_qk_a"], W["g_qk_b"], W["g_qk_c"], W["g_qk_d"], W["g_qk_x"]
        col("aq", [(0, ga[l, 0, :]), (64, ga[l, 0, :])])
        col("ak", [(0, ga[l, 1, :]), (64, ga[l, 1, :])])
        col("cq", [(0, gc[l, 0, :]), (64, gc[l, 0, :])])
        col("ck", [(0, gc[l, 1, :]), (64, gc[l, 1, :])])
        col("dq", [(0, gd[l, 0, :]), (64, gd[l, 0, :])])
        col("dk", [(0, gd[l, 1, :]), (64, gd[l, 1, :])])

        def halfswap(g1d, base, half):
            return [(base, g1d[half:2 * half]), (base + half, g1d[0:half])]

        def dpart(g1d):
            p = []
            for hb in (0, 64):
                p += [(hb + 0, g1d[16:32]), (hb + 16, g1d[0:16]), (hb + 32, g1d[48:64]), (hb + 48, g1d[32:48])]
            return p
        col("dqp", dpart(gd[l, 0, :]))
        col("dkp", dpart(gd[l, 1, :]))
        col("bq", [(0, gb[l, 0, :])])
        col("bqp", [(0, gb[l, 0, 0:64])] + halfswap(gb[l, 0, 64:96], 64, 16))
        col("bkn", [(0, gb[l, 1, 0:64])])
        col("bkr", [(0, gb[l, 1, 64:96])])
        col("bkrp", halfswap(gb[l, 1, 64:96], 0, 16))
        col("cq0", [(0, W["g_cq"][l, 0:128])])
        col("cq1", [(0, W["g_cq"][l, 128:256])])
        col("ckv", [(0, W["g_ckv"][l, :])])
        col("xq", [(0, gx[l, 0, :])])
        col("xk", [(0, gx[l, 1, :])])
        col("gsub", [(0, W["g_sub_a"][l, :])])
        self.gcols = cols
        LAM = es.enter_context(nc.sbuf_tensor("LAM%d" % l, [128, 256], F32))
        SK = es.enter_context(nc.sbuf_tensor("SK%d" % l, [128, 16], F32))
        LS = es.enter_context(nc.sbuf_tensor("LS%d" % l, [128, 8], F32))
        self.LS, self.SK = LS, SK
        s.dma("sp", LAM[:], W["lam"][l].rearrange("a b -> (a b)").partition_broadcast(128), [], ["LAM"], self.u("cl"))
        s.dma("sp", SK[:, 0:8], W["sink_c"][l, :].partition_broadcast(128), [], ["SK"], self.u("cl"))
        lam_init = 0.8 - 0.6 * math.exp(-0.3 * l)
        J = es.enter_context(nc.sbuf_tensor("LJ%d" % l, [128, 128], F32))
        s.add("dve", lambda e: e.tensor_tensor(J[:, 0:64], LAM[:, 0:64], LAM[:, 64:128], op=ALU.mult), ["LAM"], ["LJ"])
        s.add("dve", lambda e: e.tensor_tensor(J[:, 64:128], LAM[:, 128:192], LAM[:, 192:256], op=ALU.mult), ["LAM", "LJ"], ["LJ"])
        s.add("dve", lambda e: e.tensor_reduce(LS[:, 0:1], J[:, 0:64], axis=mybir.AxisListType.X, op=ALU.add), ["LJ"], ["LS"])
        s.add("dve", lambda e: e.tensor_reduce(LS[:, 1:2], J[:, 64:128], axis=mybir.AxisListType.X, op=ALU.add), ["LJ", "LS"], ["LS"])
        s.add("act", lambda e: e.activation(out=LS[:, 2:4], in_=LS[:, 0:2], func=AF.Exp), ["LS"], ["LS"])
        s.add("dve", lambda e: e.tensor_tensor(LS[:, 4:5], LS[:, 3:4], LS[:, 2:3], op=ALU.subtract), ["LS"], ["LS"])
        s.add("dve", lambda e: e.tensor_scalar(LS[:, 4:5], LS[:, 4:5], -lam_init, None, op0=ALU.add), ["LS"], ["LS"])
        gk = cols["gsub"]
        s.add("dve", lambda e: e.tensor_scalar(G[:, gk:gk + 1], G[:, gk:gk + 1], 1.0 - lam_init, None, op0=ALU.mult), ["GC"], ["GC"])
        s.add("act", lambda e: e.activation(out=SK[:, 8:16], in_=SK[:, 0:8], func=AF.Exp), ["SK"], ["SK"])

    def gcol(self, name, rows=128, r0=0):
        k = self.gcols[name]
        return self.G[r0:r0 + rows, k:k + 1]

    def norm_transpose(self, es2, items, gvec, tag):
        nc, s = self.nc, self.s
        GB = es2.enter_context(nc.sbuf_tensor("GB" + tag, [128, D], F32))
        s.dma("sp", GB[:], gvec.partition_broadcast(128), [], ["GB" + tag], self.u("cgb"))
        XIN = Ring(nc, es2, "XIN" + tag, [128, D], F32, 3)
        JK = Ring(nc, es2, "JK" + tag, [128, D], BF16, 2)
        ST = Ring(nc, es2, "ST" + tag, [128, 4], F32, 3)
        HH = Ring(nc, es2, "HH" + tag, [128, D], BF16, 2)
        for it in items:
            src, dst, dres = it[0], it[1], it[2]
            x, xr = XIN.next()
            jk, _ = JK.next()
            st, sr = ST.next()
            hh, hr = HH.next()
            pb, pr = self.bank()
            s.dma("sp", x[:], src, [], [xr], "c" + xr)
            s.add("act", lambda e, x=x, jk=jk, st=st: e.activation(out=jk[:], in_=x[:], func=AF.Square, accum_out=st[:, 0:1]), [xr], [sr])
            s.add("act", lambda e, st=st: e.activation(out=st[:, 1:2], in_=st[:, 0:1], func=AF.Ln, bias=EPS, scale=1.0 / D), [sr], [sr])
            s.add("act", lambda e, st=st: e.activation(out=st[:, 2:3], in_=st[:, 1:2], func=AF.Exp, scale=-0.5), [sr], [sr])
            s.add("dve", lambda e, x=x, st=st, hh=hh: e.scalar_tensor_tensor(out=hh[:], in0=x[:], scalar=st[:, 2:3], in1=GB[:], op0=ALU.mult, op1=ALU.mult), [xr, sr, "GB" + tag], [hr])
            pbb = pb[:].bitcast(BF16)
            for k in range(8):
                s.add("pe", lambda e, k=k, hh=hh, pbb=pbb: e.transpose(pbb[:, k * 128:(k + 1) * 128], hh[:, k * 128:(k + 1) * 128], self.IDB[:]), [hr, "IDB"], [pr])
            s.add("act", lambda e, dst=dst, pbb=pbb: e.copy(dst, pbb.rearrange("p (k t) -> p k t", k=8)), [pr], [dres])

    def job_pool(self, es2, tag):
        nc = self.nc
        self.jp = dict(
            WT=Ring(nc, es2, "WT" + tag, [128, 8, 128], BF16, 4), SQ=Ring(nc, es2, "SQ" + tag, [128, 512], F32, 3),
            LL=Ring(nc, es2, "LL" + tag, [128, 512], F32, 2), RS=Ring(nc, es2, "RS" + tag, [128, 512], F32, 2),
            OUT=Ring(nc, es2, "OUT" + tag, [128, 512], BF16, 3), CS=Ring(nc, es2, "CS" + tag, [128, 2, 512], F32, 2),
            T1=Ring(nc, es2, "T1" + tag, [128, 512], F32, 2), T2=Ring(nc, es2, "T2" + tag, [128, 512], F32, 2),
            WV=Ring(nc, es2, "WV" + tag, [128, 8, 512], BF16, 2), VO=Ring(nc, es2, "VO" + tag, [128, 512], BF16, 3))

    def qk_job(self, es2, tag, src, src_res, nk, ntok, groups, ommat, rope=None):
        nc, s = self.nc, self.s
        self.jobn = getattr(self, "jobn", 0) + 1
        if self.jobn > getattr(self, "joblimit", 10 ** 9):
            return
        P = self.jp
        WT, SQ, LL, RS, OUT = P["WT"], P["SQ"], P["LL"], P["RS"], P["OUT"]
        CS, T1, T2 = P["CS"], P["T1"], P["T2"]
        ntb = ntok // 512
        for grp in groups:
            for ch in grp:
                for key in ("w", "wp"):
                    if ch.get(key) is None:
                        continue
                    wtf, wr = WT.next()
                    wt = wtf[:, 0:nk, :]
                    ch[key + "_t"], ch[key + "_r"] = wt, wr
                    parts = ch[key]
                    if not isinstance(parts, list):
                        parts = [(0, parts)]
                    for c0, ap in parts:
                        cw = ap.shape[1]
                        self.wload(wt[:, :, c0:c0 + cw], wr, ap, self.u("cw"))
            for tb in range(ntb):
                tsl = slice(tb * 512, (tb + 1) * 512)
                if rope is not None:
                    cs, csr = CS.next()
                    R0 = rope[0].shape[0]
                    s.dma("sp", cs[0:R0, 0, :], rope[0][:, tsl], [], [csr], self.u("cr"))
                    s.dma("sp", cs[0:R0, 1, :], rope[1][:, tsl], [csr], [csr], self.u("cr"))
                pss, pssr = self.bank()
                pxs = []
                for ci, ch in enumerate(grp):
                    R = ch["R"]
                    px, pxr = self.bank()
                    for k in range(nk):
                        s.add("pe", lambda e, px=px, ch=ch, k=k, R=R, tsl=tsl: e.matmul(px[0:R, :], ch["w_t"][:, k, 0:R], src[:, k, tsl], start=(k == 0), stop=(k == nk - 1)), [ch["w_r"], src_res], [pxr])
                    pxp = pxpr = None
                    if ch.get("wp") is not None:
                        pxp, pxpr = self.bank()
                        for k in range(nk):
                            s.add("pe", lambda e, pxp=pxp, ch=ch, k=k, R=R, tsl=tsl: e.matmul(pxp[0:R, :], ch["wp_t"][:, k, 0:R], src[:, k, tsl], start=(k == 0), stop=(k == nk - 1)), [ch["wp_r"], src_res], [pxpr])
                    sq, sqr = SQ.next()
                    s.add("act", lambda e, sq=sq, px=px, R=R: e.activation(out=sq[0:R, :], in_=px[0:R, :], func=AF.Square), [pxr], [sqr])
                    Rg = grp[0]["R"]
                    s.add("pe", lambda e, pss=pss, sq=sq, R=R, ci=ci, Rg=Rg: e.matmul(pss[0:Rg, :], self.CM[0:R, ommat, 0:Rg], sq[0:R, :], start=(ci == 0), stop=(ci == len(grp) - 1)), [sqr, "CM"], [pssr])
                    pxs.append((px, pxr, pxp, pxpr))
                Rg = grp[0]["R"]
                ll, llr = LL.next()
                rs, rsr = RS.next()
                s.add("act", lambda e, ll=ll, pss=pss, Rg=Rg: e.activation(out=ll[0:Rg, :], in_=pss[0:Rg, :], func=AF.Ln, bias=EPS, scale=1.0), [pssr], [llr])
                s.add("act", lambda e, ll=ll, rs=rs, Rg=Rg: e.activation(out=rs[0:Rg, :], in_=ll[0:Rg, :], func=AF.Exp, scale=-0.5), [llr], [rsr])
                for ch, (px, pxr, pxp, pxpr) in zip(grp, pxs):
                    R = ch["R"]
                    o, orr = OUT.next()
                    if pxp is None:
                        s.add("dve", lambda e, o=o, px=px, ch=ch, rs=rs, R=R: e.scalar_tensor_tensor(out=o[0:R, :], in0=px[0:R, :], scalar=ch["g"], in1=rs[0:R, :], op0=ALU.mult, op1=ALU.mult), [pxr, rsr, "GC"], [orr])
                    else:
                        t1, t1r = T1.next()
                        t2, t2r = T2.next()
                        s.add("dve", lambda e, t1=t1, px=px, ch=ch, cs=cs, R=R: e.scalar_tensor_tensor(out=t1[0:R, :], in0=px[0:R, :], scalar=ch["g"], in1=cs[0:R, 0, :], op0=ALU.mult, op1=ALU.mult), [pxr, csr, "GC"], [t1r])
                        s.add("dve", lambda e, t2=t2, pxp=pxp, ch=ch, cs=cs, R=R: e.scalar_tensor_tensor(out=t2[0:R, :], in0=pxp[0:R, :], scalar=ch["gp"], in1=cs[0:R, 1, :], op0=ALU.mult, op1=ALU.mult), [pxpr, csr, "GC"], [t2r])
                        pe_ = ROPE_ENG
                        s.add(pe_, lambda e, t1=t1, t2=t2, R=R: e.tensor_tensor(t1[0:R, :], t1[0:R, :], t2[0:R, :], op=ALU.add), [t1r, t2r], [t1r])
                        s.add(pe_, lambda e, o=o, t1=t1, rs=rs, R=R: e.tensor_tensor(o[0:R, :], t1[0:R, :], rs[0:R, :], op=ALU.mult), [t1r, rsr], [orr])
                    if ch.get("sb_dst") is not None:
                        dstt, dres = ch["sb_dst"]
                        s.add("dve", lambda e, o=o, dstt=dstt, R=R, tsl=tsl: e.tensor_copy(dstt[0:R, tsl], o[0:R, :]), [orr], [dres])
                    for (r0, r1, dd) in ch.get("dsts", []):
                        s.dma("sp", dd[:, tsl], o[r0:r1, :], [orr], [self.u("d")], "c" + orr)

    def v_job(self, es2, tag, src, src_res, nk, wparts, ncols, dst):
        nc, s = self.nc, self.s
        self.jobn = getattr(self, "jobn", 0) + 1
        if self.jobn > getattr(self, "joblimit", 10 ** 9):
            return
        wtf, wvr = self.jp["WV"].next()
        wt = wtf[:, 0:nk, 0:ncols]
        for c0, ap in wparts:
            self.wload(wt[:, :, c0:c0 + ap.shape[1]], wvr, ap, self.u("cw"))
        VO = self.jp["VO"]
        for t in range(NK // 128):
            px, pxr = self.bank()
            for k in range(nk):
                s.add("pe", lambda e, px=px, k=k, t=t: e.matmul(px[:, 0:ncols], src[:, k, t * 128:(t + 1) * 128], wt[:, k, :], start=(k == 0), stop=(k == nk - 1)), [wvr, src_res], [pxr])
            vo, vr = VO.next()
            s.add("act", lambda e, vo=vo, px=px: e.copy(vo[:, 0:ncols], px[:, 0:ncols]), [pxr], [vr])
            s.dma("sp", dst[t * 128:(t + 1) * 128, :], vo[:, 0:ncols], [vr], [self.u("d")], "c" + vr)

    def front(self, es, l, xo_src, xf_src):
        nc, s, W = self.nc, self.s, self.W
        win = W["w_in"][l]
        self.HTO = es.enter_context(nc.sbuf_tensor("HTO%d" % l, [128, 8, NQ], BF16))
        esb = es.enter_context(ExitStack())
        CQN = esb.enter_context(nc.sbuf_tensor("CQN%d" % l, [128, 2, NQ], BF16))
        CKVN = esb.enter_context(nc.sbuf_tensor("CKVN%d" % l, [128, 1, NK], BF16))
        with ExitStack() as es1:
            HTF = es1.enter_context(nc.sbuf_tensor("HTF%d" % l, [128, 8, NK], BF16))
            with ExitStack() as es2:
                items = [(xf_src[t * 128:(t + 1) * 128, :], HTF[:, :, t * 128:(t + 1) * 128], "HTF") for t in range(32)]
                items += [(xo_src[t * 128:(t + 1) * 128, :], self.HTO[:, :, t * 128:(t + 1) * 128], "HTO") for t in range(16)]
                self.norm_transpose(es2, items, W["g_mix"][l, :], "n1")
                s.dma("sp", self.HTOs.rearrange("(k p) t -> p k t", p=128), self.HTO[:], ["HTO"], ["HTOs"], "c")
                if self.HTFd is not None:
                    s.dma("sp", self.HTFd.rearrange("(k p) t -> p k t", p=128), HTF[:], ["HTF"], ["dbg"], "cdbg2")
            s.barrier()
            g = self.gcol
            if getattr(self, "stop", None) == "norm":
                return
            esj = es1.enter_context(ExitStack())
            self.job_pool(esj, "a%d" % l)

            def wc(c0, n):
                return win[:, c0:c0 + n]
            with ExitStack() as es2:
                grp = [[dict(R=128, w=wc(C_AQ + 128 * h, 128), g=g("aq"), dsts=[(0, 128, self.AQs[128 * h:128 * h + 128, :])])] for h in range(4)]
                self.qk_job(es2, "aq", self.HTO, "HTO", 8, NQ, grp, 1)
            with ExitStack() as es2:
                grp = [[dict(R=128, w=wc(C_AK + 128 * h, 128), g=g("ak"), dsts=[(0, 128, self.AKs[128 * h:128 * h + 128, :])])] for h in range(4)]
                self.qk_job(es2, "ak", HTF, "HTF", 8, NK, grp, 1)
            with ExitStack() as es2:
                grp = [[dict(R=128, w=wc(C_CQ + 128 * j, 128), g=g("cq"), dsts=[(0, 128, self.CQs[128 * j:128 * j + 128, :])])] for j in range(4)]
                self.qk_job(es2, "cq", self.HTO, "HTO", 8, NQ, grp, 1)
            with ExitStack() as es2:
                grp = [[dict(R=128, w=wc(C_CK, 128), g=g("ck"), dsts=[(0, 128, self.CKs[:, :])])]]
                self.qk_job(es2, "ck", HTF, "HTF", 8, NK, grp, 1)

            def dpartner(c0):
                parts = []
                for hb in (0, 64):
                    parts += [(hb + 0, wc(c0 + hb + 16, 16)), (hb + 16, wc(c0 + hb + 0, 16)),
                              (hb + 32, wc(c0 + hb + 48, 16)), (hb + 48, wc(c0 + hb + 32, 16))]
                return parts
            with ExitStack() as es2:
                grp = [[dict(R=128, w=wc(C_DQ + 128 * j, 128), wp=dpartner(C_DQ + 128 * j), g=g("dq"), gp=g("dqp"),
                             dsts=[(0, 128, self.DQs[128 * j:128 * j + 128, :])])] for j in range(4)]
                self.qk_job(es2, "dq", self.HTO, "HTO", 8, NQ, grp, 1, rope=(self.rt_dq[0], self.rt_dq[1]))
            with ExitStack() as es2:
                grp = [[dict(R=128, w=wc(C_DK, 128), wp=dpartner(C_DK), g=g("dk"), gp=g("dkp"), dsts=[(0, 128, self.DKs[:, :])])]]
                self.qk_job(es2, "dk", HTF, "HTF", 8, NK, grp, 1, rope=(self.rt_dk[0], self.rt_dk[1]))
            with ExitStack() as es2:
                wp = [(0, wc(C_BKR + 16, 16)), (16, wc(C_BKR, 16))]
                grp = [[dict(R=32, w=wc(C_BKR, 32), wp=wp, g=g("bkr", 32), gp=g("bkrp", 32),
                             dsts=[(0, 32, self.BKs[h, 64:96, :]) for h in range(8)])]]
                self.qk_job(es2, "bkr", HTF, "HTF", 8, NK, grp, 5, rope=(self.rt_bk[0], self.rt_bk[1]))
            if True:
                with ExitStack() as es2:
                    grp = [[dict(R=128, w=wc(C_BCQ, 128), g=g("cq0"), sb_dst=(CQN[:, 0, :], "CQN")),
                            dict(R=128, w=wc(C_BCQ + 128, 128), g=g("cq1"), sb_dst=(CQN[:, 1, :], "CQN"))]]
                    self.qk_job(es2, "bcq", self.HTO, "HTO", 8, NQ, grp, 4)
                with ExitStack() as es2:
                    grp = [[dict(R=128, w=wc(C_BCKV, 128), g=g("ckv"), sb_dst=(CKVN[:, 0, :], "CKVN"))]]
                    self.qk_job(es2, "bckv", HTF, "HTF", 8, NK, grp, 3)
                with ExitStack() as es2:
                    self.v_job(es2, "va", HTF, "HTF", 8, [(0, wc(C_AV, 512))], 512, self.VAs)
                with ExitStack() as es2:
                    self.v_job(es2, "vcd", HTF, "HTF", 8, [(0, wc(C_CV, 128)), (128, wc(C_DV, 128))], 256, self.VCDs)
                s.barrier()
        with ExitStack() as esj2:
            self.job_pool(esj2, "b%d" % l)
            if True:
                wuq, wukv = W["w_uq"][l], W["w_ukv"][l]
                with ExitStack() as es2:
                    grp = []
                    for h in range(8):
                        c0 = 96 * h
                        wpp = [(0, wuq[:, c0:c0 + 64]), (64, wuq[:, c0 + 80:c0 + 96]), (80, wuq[:, c0 + 64:c0 + 80])]
                        grp.append([dict(R=96, w=wuq[:, c0:c0 + 96], wp=wpp, g=g("bq", 96), gp=g("bqp", 96),
                                         dsts=[(0, 96, self.BQs[h, :, :])])])
                    self.qk_job(es2, "bq", CQN, "CQN", 2, NQ, grp, 2, rope=(self.rt_bq[0], self.rt_bq[1]))
                with ExitStack() as es2:
                    grp = [[dict(R=64, w=wukv[:, 128 * h:128 * h + 64], g=g("bkn", 64), dsts=[(0, 64, self.BKs[h, 0:64, :])])] for h in range(8)]
                    self.qk_job(es2, "bkn", CKVN, "CKVN", 1, NK, grp, 1)
                with ExitStack() as es2:
                    self.v_job(es2, "vb", CKVN, "CKVN", 1, [(64 * h, wukv[:, 128 * h + 64:128 * h + 128]) for h in range(8)], 512, self.VBs)
                s.barrier()
        esb.close()

    def attn_unit(self, R, r0, KT, Kres, QT, Qres, qsl, VT, Vres, dv, tiles, scale, PO, POr, PD, PDr, PT, TMP):
        s = self.s
        n = len(tiles)
        LA = 2
        pend = []
        for i in range(n + LA):
            if i < n:
                kc, mode, ref, refres = tiles[i]
                ps, psr = self.bank()
                s.add("pe", lambda e, ps=ps, kc=kc: e.matmul(ps[:, :], KT[r0:r0 + R, kc * 128:(kc + 1) * 128], QT[r0:r0 + R, qsl], start=True, stop=True), [Kres, Qres], [psr])
                pt, ptr = PT.next()
                if mode == "tile":
                    tmp, tr = TMP.next()
                    s.add("dve", lambda e, tmp=tmp, ps=ps, ref=ref: e.scalar_tensor_tensor(out=tmp[:], in0=ps[:, :], scalar=float(scale), in1=ref, op0=ALU.mult, op1=ALU.add), [psr, refres], [tr])
                    s.add("act", lambda e, pt=pt, tmp=tmp: e.activation(out=pt[:], in_=tmp[:], func=AF.Exp), [tr], [ptr])
                elif mode == "col":
                    s.add("act", lambda e, pt=pt, ps=ps, ref=ref: e.activation(out=pt[:], in_=ps[:, :], func=AF.Exp, bias=ref, scale=float(scale)), [psr, refres], [ptr])
                else:
                    s.add("act", lambda e, pt=pt, ps=ps: e.activation(out=pt[:], in_=ps[:, :], func=AF.Exp, scale=float(scale)), [psr], [ptr])
                pend.append((kc, pt, ptr))
            if i >= LA:
                kc, pt, ptr = pend[i - LA]
                first, last = (i - LA == 0), (i - LA == n - 1)
                s.add("pe", lambda e, kc=kc, pt=pt, first=first, last=last: e.matmul(PO[0:dv, :], VT[:, kc, 0:dv], pt[:], start=first, stop=last), [Vres, ptr], [POr])
                s.add("pe", lambda e, pt=pt, first=first, last=last: e.matmul(PD[0:dv, :], self.ONESB[:, 0:dv], pt[:], start=first, stop=last), [ptr, "ONESB"], [PDr])

    def attn_finish(self, dv, PO, POr, PD, PDr, LL, RS, dst_fn, sink=None):
        s = self.s
        ll, llr = LL.next()
        rs, rsr = RS.next()
        if sink is None:
            s.add("act", lambda e, ll=ll: e.activation(out=ll[0:dv, :], in_=PD[0:dv, :], func=AF.Ln), [PDr], [llr])
        else:
            s.add("act", lambda e, ll=ll: e.activation(out=ll[0:dv, :], in_=PD[0:dv, :], func=AF.Ln, bias=sink, scale=1.0), [PDr, "SK"], [llr])
        s.add("act", lambda e, ll=ll, rs=rs: e.activation(out=rs[0:dv, :], in_=ll[0:dv, :], func=AF.Exp, scale=-1.0), [llr], [rsr])
        dst_fn(rs, rsr)

    def attention(self, es, l):
        nc, s, T = self.nc, self.s, self.T
        self.bankpool = [0, 1, 2, 3]
        acc = [(self.PS[4], "PB4", self.PS[5], "PB5"), (self.PS[6], "PB6", self.PS[7], "PB7")]
        BC = es.enter_context(nc.sbuf_tensor("BCA%d" % l, [128, 512], F32))
        s.dma("sp", BC[:], self.bca, [], ["BC"], "c")
        PT = Ring(nc, es, "PT%d" % l, [128, 512], BF16, 4)
        TMP = Ring(nc, es, "TMPat%d" % l, [128, 512], F32, 2)
        LL = Ring(nc, es, "LLat%d" % l, [128, 512], F32, 2)
        RS = Ring(nc, es, "RSat%d" % l, [128, 512], F32, 2)
        ON = Ring(nc, es, "ONat%d" % l, [128, 512], BF16, 3)
        a_idx = {k: i for i, k in enumerate(T["a_tile_keys"])}
        c_idx = {k: i for i, k in enumerate(T["c_tile_keys"])}
        ucount = [0]

        def std_dst(mix, row0, dv, qb, PO, POr):
            def f(rs, rsr):
                on, onr = ON.next()
                s.add("dve", lambda e, on=on, rs=rs: e.tensor_tensor(on[0:dv, :], PO[0:dv, :], rs[0:dv, :], op=ALU.mult), [POr, rsr], [onr])
                s.dma("sp", self.Os[mix, row0:row0 + dv, qb * 512:(qb + 1) * 512], on[0:dv, :], [onr], [self.u("d")], "c")
            return f

        with ExitStack() as es2:
            KA = Ring(nc, es2, "KA", [128, NK], BF16, 2)
            QA = Ring(nc, es2, "QA", [128, NQ], BF16, 2)
            VA = Ring(nc, es2, "VA", [128, 32, 128], BF16, 2)
            NTA = Ring(nc, es2, "NTA", [128, len(a_idx), 512], F32, 1)
            DA = es2.enter_context(nc.sbuf_tensor("DA", [128, 4, NQ], F32))
            TA = Ring(nc, es2, "TA", [128, 512], F32, 2)
            for h in range(4):
                ka, kar = KA.next()
                qa, qar = QA.next()
                va, var = VA.next()
                nt, ntr = NTA.next()
                s.dma("sp", ka[:], self.AKs[128 * h:128 * h + 128, :], [], [kar], "c")
                s.dma("sp", qa[:], self.AQs[128 * h:128 * h + 128, :], [], [qar], "c")
                s.dma("sp", va[:], self.VAs[:, 128 * h:128 * h + 128].rearrange("(k p) c -> p k c", p=128), [], [var], "c")
                s.dma("sp", nt[:], self.nta[h].rearrange("t p c -> p t c"), [], [ntr], "c")
                for qb in range(4):
                    tiles = []
                    for kc in range(32):
                        key = _class_key(qb, kc)
                        if key in a_idx:
                            tiles.append((kc, "tile", nt[:, a_idx[key], :], ntr))
                        else:
                            ci = (h * 4 + qb) * 32 + kc
                            tiles.append((kc, "col", BC[:, ci:ci + 1], "BC"))
                    tms = []
                    for m in range(2):
                        PO, POr, PD, PDr = acc[m]
                        self.attn_unit(64, 64 * m, ka, kar, qa, qar, slice(qb * 512, (qb + 1) * 512), va, var, 128, tiles, 0.125, PO, POr, PD, PDr, PT, TMP)

                        def dst(rs, rsr, PO=PO, POr=POr):
                            t, tr = TA.next()
                            s.add("dve", lambda e, t=t, rs=rs: e.tensor_tensor(t[:], PO[:, :], rs[:], op=ALU.mult), [POr, rsr], [tr])
                            tms.append((t, tr))
                        self.attn_finish(128, PO, POr, PD, PDr, LL, RS, dst)
                    (t0, t0r), (t1, t1r) = tms
                    s.add("dve", lambda e, t0=t0, t1=t1, h=h, qb=qb: e.scalar_tensor_tensor(out=DA[:, h, qb * 512:(qb + 1) * 512], in0=t1[:], scalar=self.LS[:, 4:5], in1=t0[:], op0=ALU.mult, op1=ALU.add), [t0r, t1r, "LS"], ["DA%d_%d" % (h, qb)])
            SQ = Ring(nc, es2, "SQs", [128, 512], F32, 2)
            for h in range(4):
                for qb in range(4):
                    qsl = slice(qb * 512, (qb + 1) * 512)
                    dres = "DA%d_%d" % (h, qb)
                    sq, sqr = SQ.next()
                    pss, pssr = self.bank()
                    ll, llr = LL.next()
                    rs, rsr = RS.next()
                    on, onr = ON.next()
                    s.add("act", lambda e, sq=sq, h=h, qsl=qsl: e.activation(out=sq[:], in_=DA[:, h, qsl], func=AF.Square), [dres], [sqr])
                    s.add("pe", lambda e, pss=pss, sq=sq: e.matmul(pss[:, :], self.CM[:, 3, :], sq[:], start=True, stop=True), [sqr, "CM"], [pssr])
                    s.add("act", lambda e, ll=ll, pss=pss: e.activation(out=ll[:], in_=pss[:, :], func=AF.Ln, bias=EPS, scale=1.0), [pssr], [llr])
                    s.add("act", lambda e, ll=ll, rs=rs: e.activation(out=rs[:], in_=ll[:], func=AF.Exp, scale=-0.5), [llr], [rsr])
                    s.add("dve", lambda e, on=on, rs=rs, h=h, qsl=qsl: e.scalar_tensor_tensor(out=on[:], in0=DA[:, h, qsl], scalar=self.gcol("gsub"), in1=rs[:], op0=ALU.mult, op1=ALU.mult), [dres, rsr, "GC"], [onr])
                    s.dma("sp", self.Os[0, 128 * h:128 * h + 128, qsl], on[:], [onr], [self.u("d")], "c")
        s.barrier()
        with ExitStack() as es2:
            KB = Ring(nc, es2, "KB", [128, NK], BF16, 2)
            QB = Ring(nc, es2, "QB", [128, NQ], BF16, 2)
            VB = Ring(nc, es2, "VB", [128, 32, 64], BF16, 2)
            zt = [(kc, "zero", None, None) for kc in range(32)]
            for h in range(8):
                kb, kbr = KB.next()
                qb_, qbr = QB.next()
                vb, vbr = VB.next()
                s.dma("sp", kb[0:96, :], self.BKs[h], [], [kbr], "c")
                s.dma("sp", qb_[0:96, :], self.BQs[h], [], [qbr], "c")
                s.dma("sp", vb[:], self.VBs[:, 64 * h:64 * h + 64].rearrange("(k p) c -> p k c", p=128), [], [vbr], "c")
                for qb in range(4):
                    PO, POr, PD, PDr = acc[ucount[0] % 2]
                    ucount[0] += 1
                    self.attn_unit(96, 0, kb, kbr, qb_, qbr, slice(qb * 512, (qb + 1) * 512), vb, vbr, 64, zt, 96 ** -0.5, PO, POr, PD, PDr, PT, TMP)
                    self.attn_finish(64, PO, POr, PD, PDr, LL, RS, std_dst(1, 64 * h, 64, qb, PO, POr))
        s.barrier()
        for mix, Ksrc, Qsrc, voff in ((2, self.CKs, self.CQs, 0), (3, self.DKs, self.DQs, 128)):
            with ExitStack() as es2:
                KD = Ring(nc, es2, "KD%d" % mix, [128, NK], BF16, 2)
                QD = Ring(nc, es2, "QD%d" % mix, [128, NQ], BF16, 2)
                VD = Ring(nc, es2, "VD%d" % mix, [128, 32, 64], BF16, 2)
                if mix == 2:
                    NTC = Ring(nc, es2, "NTC", [128, len(c_idx), 512], F32, 2)
                for g in range(2):
                    kd, kdr = KD.next()
                    vd, vdr = VD.next()
                    s.dma("sp", kd[0:64, :], Ksrc[64 * g:64 * g + 64, :], [], [kdr], "c")
                    s.dma("sp", kd[64:128, :], Ksrc[64 * g:64 * g + 64, :], [kdr], [kdr], "c")
                    s.dma("sp", vd[:], self.VCDs[:, voff + 64 * g:voff + 64 * g + 64].rearrange("(k p) c -> p k c", p=128), [], [vdr], "c")
                    for j in (2 * g, 2 * g + 1):
                        qd, qdr = QD.next()
                        s.dma("sp", qd[:], Qsrc[128 * j:128 * j + 128, :], [], [qdr], "c")
                        for hh in range(2):
                            h = 2 * j + hh
                            if mix == 2:
                                nt, ntr = NTC.next()
                                s.dma("sp", nt[:], self.ntc[h].rearrange("t p c -> p t c"), [], [ntr], "c")
                            for qb in range(4):
                                if mix == 2:
                                    tiles = [(kc, "tile", nt[:, c_idx[_class_key(qb, kc)], :], ntr) for kc in range(32) if _class_key(qb, kc) in c_idx]
                                    sink = self.SK[0:64, 8 + h:9 + h]
                                else:
                                    tiles = [(kc, "zero", None, None) for kc in range(32)]
                                    sink = None
                                PO, POr, PD, PDr = acc[ucount[0] % 2]
                                ucount[0] += 1
                                self.attn_unit(64, 64 * hh, kd, kdr, qd, qdr, slice(qb * 512, (qb + 1) * 512), vd, vdr, 64, tiles, 0.125, PO, POr, PD, PDr, PT, TMP)
                                self.attn_finish(64, PO, POr, PD, PDr, LL, RS, std_dst(mix, 64 * h, 64, qb, PO, POr), sink=sink)
            s.barrier()
        self.bankpool = list(range(8))

    def xkv_prep(self, es, l):
        nc, s, W = self.nc, self.s, self.W
        self.KXT = es.enter_context(nc.sbuf_tensor("KXT%d" % l, [128, 4, NMEM], BF16))
        self.VX = es.enter_context(nc.sbuf_tensor("VX%d" % l, [128, 2, 512], BF16))
        with ExitStack() as es2:
            MEMT = es2.enter_context(nc.sbuf_tensor("MEMT", [128, 8, NMEM], BF16))
            with ExitStack() as es3:
                items = [(self.mem_in[t * 128:(t + 1) * 128, :], MEMT[:, :, t * 128:(t + 1) * 128], "MEMT") for t in range(2)]
                self.norm_transpose(es3, items, W["g_mem"][l, :], "nm")
                s.barrier()
            wk = es2.enter_context(nc.sbuf_tensor("WXK", [128, 8, 512], BF16))
            wv = es2.enter_context(nc.sbuf_tensor("WXV", [128, 8, 512], BF16))
            self.wload(wk[:], "WXK", W["w_xkv"][l][:, 0:512], "c")
            self.wload(wv[:], "WXV", W["w_xkv"][l][:, 512:1024], "c")
            SQ = Ring(nc, es2, "SQx", [128, NMEM], F32, 2)
            LL = Ring(nc, es2, "LLx", [128, NMEM], F32, 2)
            for hx in range(4):
                px, pxr = self.bank()
                pss, pssr = self.bank()
                sq, sqr = SQ.next()
                ll, llr = LL.next()
                for k in range(8):
                    s.add("pe", lambda e, px=px, k=k, hx=hx: e.matmul(px[:, 0:NMEM], wk[:, k, 128 * hx:128 * hx + 128], MEMT[:, k, :], start=(k == 0), stop=(k == 7)), ["WXK", "MEMT"], [pxr])
                s.add("act", lambda e, sq=sq, px=px: e.activation(out=sq[:], in_=px[:, 0:NMEM], func=AF.Square), [pxr], [sqr])
                s.add("pe", lambda e, pss=pss, sq=sq: e.matmul(pss[:, 0:NMEM], self.CM[:, 3, :], sq[:], start=True, stop=True), [sqr, "CM"], [pssr])
                s.add("act", lambda e, ll=ll, pss=pss: e.activation(out=ll[:], in_=pss[:, 0:NMEM], func=AF.Ln, bias=EPS, scale=1.0), [pssr], [llr])
                s.add("act", lambda e, ll=ll: e.activation(out=ll[:], in_=ll[:], func=AF.Exp, scale=-0.5), [llr], [llr])
                s.add("dve", lambda e, px=px, ll=ll, hx=hx: e.scalar_tensor_tensor(out=self.KXT[:, hx, :], in0=px[:, 0:NMEM], scalar=self.gcol("xk"), in1=ll[:], op0=ALU.mult, op1=ALU.mult), [pxr, llr, "GC"], ["KXT"])
            for mc in range(2):
                px, pxr = self.bank()
                for k in range(8):
                    s.add("pe", lambda e, px=px, k=k, mc=mc: e.matmul(px[:, :], MEMT[:, k, mc * 128:(mc + 1) * 128], wv[:, k, :], start=(k == 0), stop=(k == 7)), ["WXV", "MEMT"], [pxr])
                s.add("act", lambda e, px=px, mc=mc: e.copy(self.VX[:, mc, :], px[:, :]), [pxr], ["VX"])
            s.barrier()

    def rms_tm(self, XB, xr, gb, gbres, HT, htres, t4, ST, HN):
        s = self.s
        st, sr = ST.next()
        hn, hr = HN.next()
        pb, pr = self.bank()
        s.add("act", lambda e: e.activation(out=hn[:], in_=XB[:], func=AF.Square, accum_out=st[:, 0:1]), [xr], [sr, hr])
        s.add("act", lambda e: e.activation(out=st[:, 1:2], in_=st[:, 0:1], func=AF.Ln, bias=EPS, scale=1.0 / D), [sr], [sr])
        s.add("act", lambda e: e.activation(out=st[:, 2:3], in_=st[:, 1:2], func=AF.Exp, scale=-0.5), [sr], [sr])
        s.add("dve", lambda e: e.scalar_tensor_tensor(out=hn[:], in0=XB[:], scalar=st[:, 2:3], in1=gb[:], op0=ALU.mult, op1=ALU.mult), [xr, sr, gbres], [hr])
        pbb = pb[:].bitcast(BF16)
        for k in range(8):
            s.add("pe", lambda e, k=k: e.transpose(pbb[:, k * 128:(k + 1) * 128], hn[:, k * 128:(k + 1) * 128], self.IDB[:]), [hr, "IDB"], [pr])
        s.add("act", lambda e: e.copy(HT[:, :, t4 * 128:(t4 + 1) * 128], pbb.rearrange("p (k t) -> p k t", k=8)), [pr], [htres])

    def tail(self, es, l, xo_src, x_dst):
        nc, s, W = self.nc, self.s, self.W
        with ExitStack() as es2:
            WR = Ring(nc, es2, "WR", [128, 8192], BF16, 3)
            HB = Ring(nc, es2, "HB", [128, 8, 512], BF16, 1)
            XBr = Ring(nc, es2, "XB", [128, D], F32, 4)
            OB = Ring(nc, es2, "OB", [128, 4, 512], BF16, 2)
            YT = es2.enter_context(nc.sbuf_tensor("YT", [128, 8, 512], F32))
            YB = es2.enter_context(nc.sbuf_tensor("YB", [128, 8, 512], BF16))
            SG = Ring(nc, es2, "SG", [128, 512], F32, 3)
            TMP = Ring(nc, es2, "TMPt", [128, 512], F32, 2)
            ST = Ring(nc, es2, "STt", [128, 4], F32, 3)
            HN = Ring(nc, es2, "HNt", [128, D], BF16, 2)
            H2T = es2.enter_context(nc.sbuf_tensor("H2T", [128, 8, 512], BF16))
            H3T = es2.enter_context(nc.sbuf_tensor("H3T", [128, 8, 512], BF16))
            SQ = Ring(nc, es2, "SQt", [128, 512], F32, 2)
            LL = Ring(nc, es2, "LLt", [128, 512], F32, 2)
            RS = Ring(nc, es2, "RSt", [128, 512], F32, 2)
            QX = Ring(nc, es2, "QXt", [128, 512], BF16, 2)
            PT = Ring(nc, es2, "PTt", [128, 512], BF16, 3)
            OX = es2.enter_context(nc.sbuf_tensor("OX", [128, 4, 512], BF16))
            RL = Ring(nc, es2, "RLt", [128, 512], F32, 2)
            AT = Ring(nc, es2, "ATt", [128, 8, 512], BF16, 2)
            GX = es2.enter_context(nc.sbuf_tensor("GXb", [128, D], F32))
            GM = es2.enter_context(nc.sbuf_tensor("GMb", [128, D], F32))
            s.dma("sp", GX[:], W["g_x"][l, :].partition_broadcast(128), [], ["GXb"], "c")
            s.dma("sp", GM[:], W["g_mlp"][l, :].partition_broadcast(128), [], ["GMb"], "c")

            def wslot(src, nk, cols):
                w, wr = WR.next()
                v = w[:, 0:nk * cols].rearrange("p (k c) -> p k c", k=nk)
                self.wload(v, wr, src, "c")
                return v, wr

            def resid_add(xb, xbr, px, pxr, ch):
                s.add("dve", lambda e: e.tensor_tensor(xb[:, ch * 512:(ch + 1) * 512], px[:, :], xb[:, ch * 512:(ch + 1) * 512], op=ALU.add), [pxr, xbr], [xbr])

            for tb in range(NQ // 512):
                tsl = slice(tb * 512, (tb + 1) * 512)
                hb, hbr = HB.next()
                s.dma("sp", hb[:], self.HTOs[:, tsl].rearrange("(k p) t -> p k t", p=128), [], [hbr], "c")
                xbs = []
                for t4 in range(4):
                    xb, xbr = XBr.next()
                    s.dma("sp", xb[:], xo_src[tb * 512 + t4 * 128: tb * 512 + (t4 + 1) * 128, :], [], [xbr], "c")
                    xbs.append((xb, xbr))
                for m in range(4):
                    wg, wgr = wslot(W["w_in"][l][:, C_G + 1024 * m:C_G + 1024 * (m + 1)], 8, 1024)
                    wb, wbr = wslot(W["w_branch"][l, m], 4, 1024)
                    ob, obr = OB.next()
                    s.dma("sp", ob[:], self.Os[m][:, tsl].rearrange("(f p) t -> p f t", p=128), [], [obr], "c")
                    for d in range(8):
                        pg, pgr = self.bank()
                        pbk, pbr = self.bank()
                        for k in range(8):
                            s.add("pe", lambda e, pg=pg, k=k, d=d, wg=wg, hb=hb: e.matmul(pg[:, :], wg[:, k, 128 * d:128 * d + 128], hb[:, k, :], start=(k == 0), stop=(k == 7)), [wgr, hbr], [pgr])
                        for f in range(4):
                            s.add("pe", lambda e, pbk=pbk, f=f, d=d, wb=wb, ob=ob: e.matmul(pbk[:, :], wb[:, f, 128 * d:128 * d + 128], ob[:, f, :], start=(f == 0), stop=(f == 3)), [wbr, obr], [pbr])
                        sg, sgr = SG.next()
                        s.add("act", lambda e, sg=sg, pg=pg: e.activation(out=sg[:], in_=pg[:, :], func=AF.Sigmoid), [pgr], [sgr])
                        yres = "YT%d" % d
                        if m == 0:
                            s.add("dve", lambda e, sg=sg, pbk=pbk, d=d: e.tensor_tensor(YT[:, d, :], pbk[:, :], sg[:], op=ALU.mult), [pbr, sgr], [yres])
                        else:
                            tmp, tr = TMP.next()
                            s.add("dve", lambda e, tmp=tmp, sg=sg, pbk=pbk: e.tensor_tensor(tmp[:], pbk[:, :], sg[:], op=ALU.mult), [pbr, sgr], [tr])
                            if m < 3:
                                s.add("dve", lambda e, tmp=tmp, d=d: e.tensor_tensor(YT[:, d, :], YT[:, d, :], tmp[:], op=ALU.add), [tr, yres], [yres])
                            else:
                                s.add("dve", lambda e, tmp=tmp, d=d: e.tensor_tensor(YB[:, d, :], YT[:, d, :], tmp[:], op=ALU.add), [tr, yres], ["YB%d" % d])
                wo, wor = wslot(W["w_out"][l], 8, 1024)
                for t4 in range(4):
                    xb, xbr = xbs[t4]
                    for ch in range(2):
                        px, pxr = self.bank()
                        for d in range(8):
                            s.add("pe", lambda e, px=px, d=d, t4=t4, ch=ch, wo=wo: e.matmul(px[:, :], YB[:, d, t4 * 128:(t4 + 1) * 128], wo[:, d, ch * 512:(ch + 1) * 512], start=(d == 0), stop=(d == 7)), [wor, "YB%d" % d], [pxr])
                        resid_add(xb, xbr, px, pxr, ch)
                for t4 in range(4):
                    xb, xbr = xbs[t4]
                    self.rms_tm(xb, xbr, GX, "GXb", H2T, "H2T", t4, ST, HN)
                wq, wqr = wslot(W["w_xq"][l], 8, 512)
                wxo, wxor = wslot(W["w_xo"][l], 4, 1024)
                for hx in range(4):
                    pq, pqr = self.bank()
                    for k in range(8):
                        s.add("pe", lambda e, pq=pq, k=k, hx=hx, wq=wq: e.matmul(pq[:, :], wq[:, k, 128 * hx:128 * hx + 128], H2T[:, k, :], start=(k == 0), stop=(k == 7)), [wqr, "H2T"], [pqr])
                    sq, sqr = SQ.next()
                    pss, pssr = self.bank()
                    ll, llr = LL.next()
                    rs, rsr = RS.next()
                    qx, qxr = QX.next()
                    s.add("act", lambda e, sq=sq, pq=pq: e.activation(out=sq[:], in_=pq[:, :], func=AF.Square), [pqr], [sqr])
                    s.add("pe", lambda e, pss=pss, sq=sq: e.matmul(pss[:, :], self.CM[:, 3, :], sq[:], start=True, stop=True), [sqr, "CM"], [pssr])
                    s.add("act", lambda e, ll=ll, pss=pss: e.activation(out=ll[:], in_=pss[:, :], func=AF.Ln, bias=EPS, scale=1.0), [pssr], [llr])
                    s.add("act", lambda e, ll=ll, rs=rs: e.activation(out=rs[:], in_=ll[:], func=AF.Exp, scale=-0.5), [llr], [rsr])
                    s.add("dve", lambda e, qx=qx, pq=pq, rs=rs: e.scalar_tensor_tensor(out=qx[:], in0=pq[:, :], scalar=self.gcol("xq"), in1=rs[:], op0=ALU.mult, op1=ALU.mult), [pqr, rsr, "GC"], [qxr])
                    po, por = self.bank()
                    pd, pdr = self.bank()
                    pts = []
                    for mc in range(2):
                        ps, psr = self.bank()
                        pt, ptr = PT.next()
                        s.add("pe", lambda e, ps=ps, mc=mc, hx=hx, qx=qx: e.matmul(ps[:, :], self.KXT[:, hx, mc * 128:(mc + 1) * 128], qx[:], start=True, stop=True), ["KXT", qxr], [psr])
                        s.add("act", lambda e, pt=pt, ps=ps: e.activation(out=pt[:], in_=ps[:, :], func=AF.Exp, scale=128 ** -0.5), [psr], [ptr])
                        pts.append((pt, ptr))
                    for mc, (pt, ptr) in enumerate(pts):
                        s.add("pe", lambda e, po=po, mc=mc, hx=hx, pt=pt: e.matmul(po[:, :], self.VX[:, mc, 128 * hx:128 * hx + 128], pt[:], start=(mc == 0), stop=(mc == 1)), ["VX", ptr], [por])
                        s.add("pe", lambda e, pd=pd, mc=mc, pt=pt: e.matmul(pd[:, :], self.ONESB[:], pt[:], start=(mc == 0), stop=(mc == 1)), ["ONESB", ptr], [pdr])
                    ll2, ll2r = LL.next()
                    rs2, rs2r = RS.next()
                    s.add("act", lambda e, ll2=ll2, pd=pd: e.activation(out=ll2[:], in_=pd[:, :], func=AF.Ln), [pdr], [ll2r])
                    s.add("act", lambda e, ll2=ll2, rs2=rs2: e.activation(out=rs2[:], in_=ll2[:], func=AF.Exp, scale=-1.0), [ll2r], [rs2r])
                    s.add("dve", lambda e, po=po, rs2=rs2, hx=hx: e.tensor_tensor(OX[:, hx, :], po[:, :], rs2[:], op=ALU.mult), [por, rs2r], ["OX%d" % hx])
                for t4 in range(4):
                    xb, xbr = xbs[t4]
                    for ch in range(2):
                        px, pxr = self.bank()
                        for hx in range(4):
                            s.add("pe", lambda e, px=px, hx=hx, t4=t4, ch=ch, wxo=wxo: e.matmul(px[:, :], OX[:, hx, t4 * 128:(t4 + 1) * 128], wxo[:, hx, ch * 512:(ch + 1) * 512], start=(hx == 0), stop=(hx == 3)), [wxor, "OX%d" % hx], [pxr])
                        resid_add(xb, xbr, px, pxr, ch)
                for t4 in range(4):
                    xb, xbr = xbs[t4]
                    self.rms_tm(xb, xbr, GM, "GMb", H3T, "H3T", t4, ST, HN)
                for fg in range(4):
                    wu, wur = wslot(W["w_up"][l][:, 1024 * fg:1024 * (fg + 1)], 8, 1024)
                    wd, wdr = wslot(W["w_down"][l][1024 * fg:1024 * (fg + 1), :], 8, 1024)
                    at, atr = AT.next()
                    for f in range(8):
                        pu, pur = self.bank()
                        for k in range(8):
                            s.add("pe", lambda e, pu=pu, k=k, f=f, wu=wu: e.matmul(pu[:, :], wu[:, k, 128 * f:128 * f + 128], H3T[:, k, :], start=(k == 0), stop=(k == 7)), [wur, "H3T"], [pur])
                        rl, rlr = RL.next()
                        s.add("act", lambda e, rl=rl, pu=pu: e.activation(out=rl[:], in_=pu[:, :], func=AF.Relu), [pur], [rlr])
                        s.add("dve", lambda e, rl=rl, at=at, f=f: e.tensor_tensor(at[:, f, :], rl[:], rl[:], op=ALU.mult), [rlr], [atr + "_%d" % f])
                    for t4 in range(4):
                        xb, xbr = xbs[t4]
                        for ch in range(2):
                            px, pxr = self.bank()
                            for f in range(8):
                                s.add("pe", lambda e, px=px, f=f, t4=t4, ch=ch, wd=wd, at=at: e.matmul(px[:, :], at[:, f, t4 * 128:(t4 + 1) * 128], wd[:, f, ch * 512:(ch + 1) * 512], start=(f == 0), stop=(f == 7)), [wdr, atr + "_%d" % f], [pxr])
                            resid_add(xb, xbr, px, pxr, ch)
                for t4 in range(4):
                    xb, xbr = xbs[t4]
                    s.dma("sp", x_dst[tb * 512 + t4 * 128: tb * 512 + (t4 + 1) * 128, :], xb[:], [xbr], [self.u("d")], "c")
            s.barrier()

    def build(self, stop_after=None):
        s = self.s
        with ExitStack() as es:
            self.declare()
            self.setup_globals(es)
            for li, l in enumerate(self.layers):
                last = li == len(self.layers) - 1
                first = li == 0
                xo_src = self.xo_in if first else self.XO
                xf_src = self.xf_in if first else self.XF
                x_dst = self.out if last else self.XO
                with ExitStack() as el:
                    self.load_layer_params(el, l)
                    with ExitStack() as ef:
                        self.front(ef, l, xo_src, xf_src)
                    if stop_after == "front":
                        break
                    with ExitStack() as ea:
                        self.attention(ea, l)
                    if stop_after == "attn":
                        break
                    with ExitStack() as et:
                        self.xkv_prep(et, l)
                        self.tail(et, l, xo_src, x_dst)
                if not last:
                    self.exchange()
            s.emit()

    def exchange(self):
        s = self.s
        s.barrier()
        s.add("dve", lambda e: e.collective_compute("AllGather", ALU.bypass, replica_groups=[[0, 1], [2, 3], [4, 5], [6, 7]],
                                                     ins=[self.XO[:, :]], outs=[self.XF[:, :]]), ["XO"], ["XF"], chan="cc")
        s.barrier()


WNAMES = ["g_mix", "w_in", "lam", "g_qk_a", "g_sub_a", "g_cq", "g_ckv", "w_uq", "w_ukv", "g_qk_b", "g_qk_c", "sink_c",
          "g_qk_d", "w_branch", "w_out", "g_x", "g_mem", "w_xq", "w_xkv", "g_qk_x", "w_xo", "g_mlp", "w_up", "w_down"]


def _core_tables(c, rel_bias):
    T = _host_tables()
    qpos, kpos = T["pos"][c]
    out = {}

    def axial(pos):
        cr, sr = _rope_tables(pos // 64, 32)
        cc, sc = _rope_tables(pos % 64, 32)
        cos = np.concatenate([cr, cc], 1)
        sin = np.concatenate([sr, sc], 1)
        return np.stack([np.concatenate([cos, cos], 1).T, np.concatenate([sin, sin], 1).T]).astype(np.float32)
    out["rt_dq"] = np.ascontiguousarray(axial(qpos))
    out["rt_dk"] = np.ascontiguousarray(axial(kpos))
    cq, sq = _rope_tables(qpos, 32)
    ck, sk = _rope_tables(kpos, 32)
    bq = np.zeros((2, 96, NQ), np.float32)
    bq[0, :64] = 1.0
    bq[0, 64:] = cq.T
    bq[1, 64:] = sq.T
    out["rt_bq"] = bq
    out["rt_bk"] = np.ascontiguousarray(np.stack([ck.T, sk.T]).astype(np.float32))
    bucket, rel = T["bucket"], T["rel"]
    rel_bias = np.asarray(rel_bias, np.float32)
    bca = np.zeros((128, 4 * 4 * 32), np.float32)
    for h in range(4):
        for j in range(4):
            for kc in range(32):
                key = _class_key(j, kc)
                if key not in T["a_tile_keys"]:
                    bca[:, (h * 4 + j) * 32 + kc] = rel_bias[bucket[rel[(c, key)][0, 0] + SEQ - 1], h]
    out["bca"] = bca
    nta = np.zeros((4, len(T["a_tile_keys"]), 128, 512), np.float32)
    for i, key in enumerate(T["a_tile_keys"]):
        b = bucket[rel[(c, key)] + SEQ - 1]
        for h in range(4):
            nta[h, i] = rel_bias[b, h]
    out["nta"] = nta
    ntc = np.full((8, len(T["c_tile_keys"]), 128, 512), NEG, np.float32)
    for i, key in enumerate(T["c_tile_keys"]):
        r = rel[(c, key)]
        b = bucket[r + SEQ - 1]
        valid = np.abs(r) <= 128
        for h in range(8):
            ntc[h, i] = np.where(valid, rel_bias[b, 4 + h], np.float32(NEG))
    out["ntc"] = ntc
    return out


def _split_tokens(xb, c):
    blocks = xb.reshape(32, 128, D)
    own = blocks[c::2].reshape(NQ, D)
    kord = np.concatenate([blocks[0::2], blocks[1::2]], 0).reshape(NK, D)
    return np.ascontiguousarray(own), np.ascontiguousarray(kord)


def _in_maps(x_cur, inputs, tabs, cores):
    consts = _consts()
    maps = []
    for core in cores:
        b, c = core // 2, core % 2
        own, kord = _split_tokens(x_cur[b], c)
        m = dict(xo=own, xf=kord, mem=np.ascontiguousarray(inputs["mem"][b]), consts=consts,
                 rel_bias=np.asarray(inputs["rel_bias"], np.float32))
        for nm in WNAMES:
            m[nm] = inputs[nm]
        m.update(tabs[c])
        maps.append(m)
    return maps


_PROG_CACHE = {}


def _program(layers, fused):
    key = (tuple(layers), fused)
    if key not in _PROG_CACHE:
        nc = bass.Bass("TRN2", target_bir_lowering=False)
        b = Builder(nc, list(layers), fused)
        b.build()
        _PROG_CACHE[key] = nc
    return _PROG_CACHE[key]


FUSED = False


def kernel(**inputs):
    inputs = {k: np.ascontiguousarray(np.asarray(v, np.float32)) for k, v in inputs.items()}
    x = inputs["x"]
    tabs = [_core_tables(c, inputs["rel_bias"]) for c in range(2)]
    cores = list(range(8))
    progs = [[0, 1]] if FUSED else [[0], [1]]
    x_cur = x
    for layers in progs:
        nc = _program(layers, FUSED)
        res = run_bass_kernel_spmd(nc, _in_maps(x_cur, inputs, tabs, cores), core_ids=cores)
        nxt = np.empty_like(x)
        for core in cores:
            b, c = core // 2, core % 2
            nxt[b].reshape(32, 128, D)[c::2] = res.results[core]["out"].reshape(16, 128, D)
        x_cur = nxt
    return x_cur
```
